# Optimizing a Trainium2 kernel written in Bass

```python
import math
import jax, jax.numpy as jnp
from jax import lax
import numpy as np

D_MODEL = 2048
BATCH = 1
SEQ = 16384
DEPTH = 2

GRID_W = 64
CTX_LEN = 256
Q_BLOCK = 128
ROPE_BASE = 10000.0
LN_EPS = 1e-5
RMS_EPS = 1e-6

DA_HEADS = 4
DA_HEAD_DIM = 64
DA_WIDTH = DA_HEADS * 2 * DA_HEAD_DIM

LRU_WIDTH = 512
LRU_BLOCKS = 4
LRU_BLOCK = LRU_WIDTH // LRU_BLOCKS
CONV_W = 4
CONV_LEFT = 2
LRU_C = 8.0

MLA_HEADS = 4
MLA_Q_RANK = 384
MLA_KV_RANK = 256
MLA_NOPE = 128
MLA_ROPE = 64
MLA_V = 128
MLA_WIDTH = MLA_HEADS * MLA_V

FFT_GROUPS = 4
FFT_GROUP = 128
FFT_WIDTH = FFT_GROUPS * FFT_GROUP

MIX_WIDTH = DA_WIDTH + LRU_WIDTH + MLA_WIDTH + FFT_WIDTH

IN_DA = 3 * DA_WIDTH
IN_LRU = 2 * LRU_WIDTH
IN_MLA = MLA_Q_RANK + MLA_KV_RANK + MLA_ROPE
IN_FFT = FFT_WIDTH
OFF_LRU = IN_DA
OFF_MLA = OFF_LRU + IN_LRU
OFF_FFT = OFF_MLA + IN_MLA
IN_WIDTH = OFF_FFT + IN_FFT

D_FF = ((8 * D_MODEL + 3 * 256 - 1) // (3 * 256)) * 256

DEEPNORM_ALPHA = (2 * DEPTH) ** 0.25
DEEPNORM_BETA = (8 * DEPTH) ** -0.25

kernel_name = "hymba_style_hybrid_dit_block"


def _layernorm(x, g, b):
    xf = x.astype(jnp.float32)
    mu = jnp.mean(xf, axis=-1, keepdims=True)
    var = jnp.mean(jnp.square(xf - mu), axis=-1, keepdims=True)
    return ((xf - mu) * lax.rsqrt(var + LN_EPS) * g + b).astype(x.dtype)


def _rmsnorm(x, g):
    xf = x.astype(jnp.float32)
    inv = lax.rsqrt(jnp.mean(jnp.square(xf), axis=-1, keepdims=True) + RMS_EPS)
    return (xf * inv * g).astype(x.dtype)


def _modulate(x, shift, scale):
    return x * (1.0 + scale) + shift


def _ada_mods(cvec, w, b):
    m = jax.nn.silu(cvec) @ w + b
    return jnp.split(m, 6, axis=-1)


def _axial_rope_tables(n, rot_dim):
    rows = n // GRID_W
    row = jnp.repeat(jnp.arange(rows, dtype=jnp.float32), GRID_W)
    col = jnp.tile(jnp.arange(GRID_W, dtype=jnp.float32), rows)
    axis_dim = rot_dim // 2
    inv = ROPE_BASE ** (-jnp.arange(0, axis_dim, 2, dtype=jnp.float32) / axis_dim)
    ang = jnp.concatenate([row[:, None] * inv, col[:, None] * inv], axis=-1)
    return jnp.cos(ang), jnp.sin(ang)


def _rope(x, cos, sin):
    half = x.shape[-1] // 2
    bshape = (1, x.shape[1]) + (1,) * (x.ndim - 3) + (half,)
    c = cos.reshape(bshape)
    s = sin.reshape(bshape)
    x1 = x[..., :half].astype(jnp.float32)
    x2 = x[..., half:].astype(jnp.float32)
    return jnp.concatenate([x1 * c - x2 * s, x1 * s + x2 * c], axis=-1).astype(x.dtype)


def _sweep_query_blocks(fn, qs):
    b, n = qs[0].shape[:2]
    nb = n // Q_BLOCK
    blk = tuple(jnp.moveaxis(q.reshape((b, nb, Q_BLOCK) + q.shape[2:]), 1, 0) for q in qs)
    out = lax.map(lambda a: fn(*a), blk)
    out = jnp.moveaxis(out, 0, 1)
    return out.reshape((b, n) + out.shape[3:])


def _probs(q, k, scale):
    s = jnp.einsum("bqhd,bkhd->bhqk", q, k, preferred_element_type=jnp.float32) * scale
    return jax.nn.softmax(s, axis=-1)


def _diff_attention(u_lat, u_ctx, cos, sin, lq1, lk1, lq2, lk2, subln_g, lambda_init, need_ctx):
    def split(u):
        b, n, _ = u.shape
        q = u[..., :DA_WIDTH].reshape(b, n, DA_HEADS, 2, DA_HEAD_DIM)
        k = u[..., DA_WIDTH:2 * DA_WIDTH].reshape(b, n, DA_HEADS, 2, DA_HEAD_DIM)
        v = u[..., 2 * DA_WIDTH:].reshape(b, n, DA_HEADS, 2 * DA_HEAD_DIM)
        return q, k, v

    q_l, k_l, v_l = split(u_lat)
    q_c, k_c, v_c = split(u_ctx)
    q_l = _rope(q_l, cos, sin)
    k_l = _rope(k_l, cos, sin)
    f32 = jnp.float32
    lam = (jnp.exp(jnp.sum(lq1.astype(f32) * lk1.astype(f32)))
           - jnp.exp(jnp.sum(lq2.astype(f32) * lk2.astype(f32))) + lambda_init)
    scale = DA_HEAD_DIM ** -0.5
    k_all = jnp.concatenate([k_c, k_l], axis=1)
    v_all = jnp.concatenate([v_c, v_l], axis=1)

    def core(q, k, v):
        p = _probs(q[..., 0, :], k[..., 0, :], scale) - lam * _probs(q[..., 1, :], k[..., 1, :], scale)
        return jnp.einsum("bhqk,bkhd->bqhd", p, v.astype(f32))

    def post(o):
        b, n = o.shape[:2]
        return (_rmsnorm(o, subln_g) * (1.0 - lambda_init)).reshape(b, n, DA_WIDTH)

    y_lat = post(_sweep_query_blocks(lambda qb: core(qb, k_all, v_all), (q_l,)))
    y_ctx = post(core(q_c, k_c, v_c)) if need_ctx else None
    return y_lat, y_ctx


def _depthwise_conv(x, w, b):
    rhs = w.reshape(CONV_W, 1, x.shape[-1]).astype(x.dtype)
    y = lax.conv_general_dilated(x, rhs, window_strides=(1,),
                                 padding=[(CONV_LEFT, CONV_W - 1 - CONV_LEFT)],
                                 dimension_numbers=("NWC", "WIO", "NWC"),
                                 feature_group_count=x.shape[-1])
    return y + b


def _linear_combine(e1, e2):
    return e1[0] * e2[0], e2[0] * e1[1] + e2[1]


def _rglru_scan(x, h0, w_r, b_r, w_i, b_i, lam, reverse):
    b, n, _ = x.shape
    xb = x.reshape(b, n, LRU_BLOCKS, LRU_BLOCK)

    def gate(w, bias):
        z = jnp.einsum("bnhi,hij->bnhj", xb, w, preferred_element_type=jnp.float32)
        return jax.nn.sigmoid(z.reshape(b, n, LRU_WIDTH) + bias)

    r = gate(w_r, b_r)
    i = gate(w_i, b_i)
    log_a = -LRU_C * r * jax.nn.softplus(-lam.astype(jnp.float32))
    a = jnp.exp(log_a)
    u = jnp.sqrt(-jnp.expm1(2.0 * log_a)) * (i * x.astype(jnp.float32))
    if reverse:
        a, u = jnp.flip(a, 1), jnp.flip(u, 1)
    u = u.at[:, 0].add(a[:, 0] * h0)
    _, h = lax.associative_scan(_linear_combine, (a, u), axis=1)
    h_last = h[:, -1]
    if reverse:
        h = jnp.flip(h, 1)
    return h, h_last


def _rglru_mixer(u_lat, u_ctx, conv_w, conv_b, wr, br, wi, bi, lam, need_ctx):
    x_l = _depthwise_conv(u_lat[..., :LRU_WIDTH], conv_w, conv_b)
    x_c = _depthwise_conv(u_ctx[..., :LRU_WIDTH], conv_w, conv_b)
    h0 = jnp.zeros((u_ctx.shape[0], LRU_WIDTH), jnp.float32)
    hc_f, s_f = _rglru_scan(x_c, h0, wr[0], br[0], wi[0], bi[0], lam[0], reverse=False)
    hc_b, s_b = _rglru_scan(x_c, h0, wr[1], br[1], wi[1], bi[1], lam[1], reverse=True)
    hl_f, _ = _rglru_scan(x_l, s_f, wr[0], br[0], wi[0], bi[0], lam[0], reverse=False)
    hl_b, _ = _rglru_scan(x_l, s_b, wr[1], br[1], wi[1], bi[1], lam[1], reverse=True)
    y_lat = (hl_f + hl_b) * jax.nn.gelu(u_lat[..., LRU_WIDTH:].astype(jnp.float32))
    y_ctx = ((hc_f + hc_b) * jax.nn.gelu(u_ctx[..., LRU_WIDTH:].astype(jnp.float32))) if need_ctx else None
    return y_lat, y_ctx


def _mla(u_lat, u_ctx, cos, sin, qn_g, w_uq, kvn_g, w_ukv, need_ctx):
    def qkv(u, rotate):
        b, n, _ = u.shape
        c_q = u[..., :MLA_Q_RANK]
        c_kv = u[..., MLA_Q_RANK:MLA_Q_RANK + MLA_KV_RANK]
        k_r = u[..., MLA_Q_RANK + MLA_KV_RANK:][:, :, None, :]
        q = (_rmsnorm(c_q, qn_g) @ w_uq).reshape(b, n, MLA_HEADS, MLA_NOPE + MLA_ROPE)
        kv = (_rmsnorm(c_kv, kvn_g) @ w_ukv).reshape(b, n, MLA_HEADS, MLA_NOPE + MLA_V)
        q_nope, q_rope = q[..., :MLA_NOPE], q[..., MLA_NOPE:]
        k_nope, v = kv[..., :MLA_NOPE], kv[..., MLA_NOPE:]
        if rotate:
            q_rope = _rope(q_rope, cos, sin)
            k_r = _rope(k_r, cos, sin)
        q = jnp.concatenate([q_nope, q_rope], axis=-1)
        k = jnp.concatenate([k_nope, jnp.broadcast_to(k_r, (b, n, MLA_HEADS, MLA_ROPE))], axis=-1)
        return q, k, v

    q_l, k_l, v_l = qkv(u_lat, True)
    q_c, k_c, v_c = qkv(u_ctx, False)
    scale = (MLA_NOPE + MLA_ROPE) ** -0.5
    k_all = jnp.concatenate([k_c, k_l], axis=1)
    v_all = jnp.concatenate([v_c, v_l], axis=1)

    def core(q, k, v):
        return jnp.einsum("bhqk,bkhd->bqhd", _probs(q, k, scale), v.astype(jnp.float32))

    b, n = u_lat.shape[:2]
    y_lat = _sweep_query_blocks(lambda qb: core(qb, k_all, v_all), (q_l,)).reshape(b, n, MLA_WIDTH)
    y_ctx = core(q_c, k_c, v_c).reshape(b, u_ctx.shape[1], MLA_WIDTH) if need_ctx else None
    return y_lat, y_ctx


def _fourier(u):
    b, n, _ = u.shape
    g = u.astype(jnp.float32).reshape(b, n, FFT_GROUPS, FFT_GROUP)
    return jnp.real(jnp.fft.fft2(g, axes=(1, 3), norm="ortho")).reshape(b, n, FFT_WIDTH)


def _swiglu(h, wg, wu, wd):
    return (jax.nn.silu(h @ wg) * (h @ wu)) @ wd


def setup_inputs(seed: int = 0) -> dict:
    key = jax.random.key(seed)
    ks = iter(jax.random.split(key, 32))

    def nrm(shape, s):
        return jax.random.normal(next(ks), shape, jnp.float32) * s

    L, D = DEPTH, D_MODEL
    inp = {}
    inp["x"] = nrm((BATCH, SEQ, D), 1.0)
    inp["c"] = nrm((BATCH, D), 1.0)
    inp["ctx"] = nrm((BATCH, CTX_LEN, D), 1.0)
    inp["c_ctx"] = nrm((D,), 1.0)
    inp["w_ada"] = nrm((L, D, 6 * D), 0.5 * D ** -0.5)
    inp["b_ada"] = nrm((L, 6 * D), 0.02)
    inp["w_in"] = nrm((L, D, IN_WIDTH), D ** -0.5)
    inp["w_out"] = nrm((L, MIX_WIDTH, D), DEEPNORM_BETA * MIX_WIDTH ** -0.5)
    inp["ln1_g"] = 1.0 + nrm((L, D), 0.02)
    inp["ln1_b"] = nrm((L, D), 0.02)
    inp["ln2_g"] = 1.0 + nrm((L, D), 0.02)
    inp["ln2_b"] = nrm((L, D), 0.02)
    inp["da_lq1"] = nrm((L, DA_HEAD_DIM), 0.1)
    inp["da_lk1"] = nrm((L, DA_HEAD_DIM), 0.1)
    inp["da_lq2"] = nrm((L, DA_HEAD_DIM), 0.1)
    inp["da_lk2"] = nrm((L, DA_HEAD_DIM), 0.1)
    inp["da_subln_g"] = 1.0 + nrm((L, 2 * DA_HEAD_DIM), 0.02)
    inp["lru_conv_w"] = nrm((L, CONV_W, LRU_WIDTH), CONV_W ** -0.5)
    inp["lru_conv_b"] = nrm((L, LRU_WIDTH), 0.02)
    inp["lru_wr"] = nrm((L, 2, LRU_BLOCKS, LRU_BLOCK, LRU_BLOCK), LRU_BLOCK ** -0.5)
    inp["lru_br"] = nrm((L, 2, LRU_WIDTH), 0.02)
    inp["lru_wi"] = nrm((L, 2, LRU_BLOCKS, LRU_BLOCK, LRU_BLOCK), LRU_BLOCK ** -0.5)
    inp["lru_bi"] = nrm((L, 2, LRU_WIDTH), 0.02)
    a_c = jax.random.uniform(next(ks), (L, 2, LRU_WIDTH), jnp.float32, 0.9, 0.999)
    a = a_c ** (1.0 / LRU_C)
    inp["lru_lam"] = jnp.log(a) - jnp.log1p(-a)
    inp["mla_qn_g"] = 1.0 + nrm((L, MLA_Q_RANK), 0.02)
    inp["mla_wuq"] = nrm((L, MLA_Q_RANK, MLA_HEADS * (MLA_NOPE + MLA_ROPE)), MLA_Q_RANK ** -0.5)
    inp["mla_kvn_g"] = 1.0 + nrm((L, MLA_KV_RANK), 0.02)
    inp["mla_wukv"] = nrm((L, MLA_KV_RANK, MLA_HEADS * (MLA_NOPE + MLA_V)), MLA_KV_RANK ** -0.5)
    inp["ffn_wg"] = nrm((L, D, D_FF), D ** -0.5)
    inp["ffn_wu"] = nrm((L, D, D_FF), D ** -0.5)
    inp["ffn_wd"] = nrm((L, D_FF, D), DEEPNORM_BETA * D_FF ** -0.5)
    return inp


def reference(x, c, ctx, c_ctx, w_ada, b_ada, w_in, w_out, ln1_g, ln1_b, ln2_g, ln2_b,
              da_lq1, da_lk1, da_lq2, da_lk2, da_subln_g,
              lru_conv_w, lru_conv_b, lru_wr, lru_br, lru_wi, lru_bi, lru_lam,
              mla_qn_g, mla_wuq, mla_kvn_g, mla_wukv,
              ffn_wg, ffn_wu, ffn_wd):
    n = x.shape[1]
    cos_da, sin_da = _axial_rope_tables(n, DA_HEAD_DIM)
    cos_mla, sin_mla = _axial_rope_tables(n, MLA_ROPE)
    x_lat, x_ctx = x, ctx
    for l in range(DEPTH):
        need_ctx = l < DEPTH - 1
        lambda_init = 0.8 - 0.6 * math.exp(-0.3 * l)
        sh1, sc1, g1, sh2, sc2, g2 = [m[:, None, :] for m in _ada_mods(c, w_ada[l], b_ada[l])]
        csh1, csc1, cg1, csh2, csc2, cg2 = _ada_mods(c_ctx, w_ada[l], b_ada[l])

        u_lat = _modulate(x_lat, sh1, sc1) @ w_in[l]
        u_ctx = _modulate(x_ctx, csh1, csc1) @ w_in[l]
        da_l, da_c = _diff_attention(u_lat[..., :OFF_LRU], u_ctx[..., :OFF_LRU], cos_da, sin_da,
                                     da_lq1[l], da_lk1[l], da_lq2[l], da_lk2[l], da_subln_g[l],
                                     lambda_init, need_ctx)
        lru_l, lru_c = _rglru_mixer(u_lat[..., OFF_LRU:OFF_MLA], u_ctx[..., OFF_LRU:OFF_MLA],
                                    lru_conv_w[l], lru_conv_b[l], lru_wr[l], lru_br[l],
                                    lru_wi[l], lru_bi[l], lru_lam[l], need_ctx)
        mla_l, mla_c = _mla(u_lat[..., OFF_MLA:OFF_FFT], u_ctx[..., OFF_MLA:OFF_FFT], cos_mla, sin_mla,
                            mla_qn_g[l], mla_wuq[l], mla_kvn_g[l], mla_wukv[l], need_ctx)
        fft_l = _fourier(u_lat[..., OFF_FFT:])
        mix_l = jnp.concatenate([da_l, lru_l, mla_l, fft_l], axis=-1).astype(x_lat.dtype)
        x_lat = _layernorm(DEEPNORM_ALPHA * x_lat + g1 * (mix_l @ w_out[l]), ln1_g[l], ln1_b[l])

        ff_l = _swiglu(_modulate(x_lat, sh2, sc2), ffn_wg[l], ffn_wu[l], ffn_wd[l])
        x_lat = _layernorm(DEEPNORM_ALPHA * x_lat + g2 * ff_l, ln2_g[l], ln2_b[l])

        if need_ctx:
            fft_c = _fourier(u_ctx[..., OFF_FFT:])
            mix_c = jnp.concatenate([da_c, lru_c, mla_c, fft_c], axis=-1).astype(x_ctx.dtype)
            x_ctx = _layernorm(DEEPNORM_ALPHA * x_ctx + cg1 * (mix_c @ w_out[l]), ln1_g[l], ln1_b[l])
            ff_c = _swiglu(_modulate(x_ctx, csh2, csc2), ffn_wg[l], ffn_wu[l], ffn_wd[l])
            x_ctx = _layernorm(DEEPNORM_ALPHA * x_ctx + cg2 * ff_c, ln2_g[l], ln2_b[l])
    return x_lat
```

```python
import numpy as np
import concourse.bass as bass
import concourse.mybir as mybir
from concourse.bass_utils import run_bass_kernel_spmd

F32 = mybir.dt.float32
BF16 = mybir.dt.bfloat16
AF = mybir.ActivationFunctionType
ALU = mybir.AluOpType
AX = mybir.AxisListType


class Buf:
    __slots__ = ("name", "w", "r", "psum")

    def __init__(self, name, psum=False):
        self.name = name
        self.w = None
        self.r = []
        self.psum = psum


class V:
    __slots__ = ("ap", "bufs")

    def __init__(self, ap, bufs):
        self.ap = ap
        self.bufs = bufs

    def __getitem__(self, idx):
        return V(self.ap[idx], self.bufs)

    def re(self, pat, **kw):
        return V(self.ap.rearrange(pat, **kw), self.bufs)

    def bc(self, shape):
        return V(self.ap.to_broadcast(shape), self.bufs)

    def with_ap(self, ap):
        return V(ap, self.bufs)


EPOCH = 16000
NDMASEM = 8
SAME_ENG_SYNC = ("act", "dve", "pool")


class Prog:
    ENGS = ["pe", "act", "dve", "pool", "sp"]

    def __init__(self, nc):
        self.nc = nc
        self.ops = []
        self.eng_ops = {e: [] for e in self.ENGS}
        self.dma_pool = {}
        self.dma_rr = {}
        self.dma_last = {}
        self.dma_cum = {}
        self.nsem = 0

    def sem(self, name):
        self.nsem += 1
        return self.nc.alloc_semaphore(name)

    def sb(self, name, shape, dtype=F32):
        t = self.nc.alloc_sbuf_tensor(name, list(shape), dtype)
        return V(t.ap() if hasattr(t, "ap") else t[:], [Buf(name)])

    def ps(self, name, shape, dtype=F32):
        t = self.nc.alloc_psum_tensor(name, list(shape), dtype)
        return V(t.ap() if hasattr(t, "ap") else t[:], [Buf(name, psum=True)])

    def dram(self, name, shape, dtype, kind):
        t = self.nc.dram_tensor(name, list(shape), dtype, kind=kind)
        return V(t.ap(), [Buf(name)])

    def _rec(self, eng, fn, reads, writes, dma=False, ndma=1):
        deps = set()
        rb, wb = [], []
        for v in reads:
            for b in v.bufs:
                if b not in rb:
                    rb.append(b)
        for v in writes:
            for b in v.bufs:
                if b not in wb:
                    wb.append(b)
        for b in rb:
            if b.w is not None:
                deps.add(b.w)
            if b.psum:
                deps.update(r for r in b.r if self.ops[r]["eng"] != eng)
        for b in wb:
            if b.w is not None:
                deps.add(b.w)
            deps.update(b.r)
        oid = len(self.ops)
        op = dict(id=oid, eng=eng, fn=fn, deps=deps, dma=dma, ndma=ndma, needed=False)
        if dma:
            if eng not in self.dma_pool:
                self.dma_pool[eng] = [self.sem(f"dq_{eng}_{i}") for i in range(NDMASEM)]
                self.dma_rr[eng] = 0
            k = self.dma_rr[eng] % NDMASEM
            self.dma_rr[eng] += 1
            key = (eng, k)
            prev = self.dma_last.get(key)
            if prev is not None:
                deps.add(prev)
            self.dma_last[key] = oid
            cum = self.dma_cum.get(key, 0) + 16 * ndma
            self.dma_cum[key] = cum
            op["sem"] = self.dma_pool[eng][k]
            op["val"] = cum
        self.ops.append(op)
        for b in rb:
            b.r.append(oid)
        for b in wb:
            b.w = oid
            b.r = []
        op["pos"] = len(self.eng_ops[eng])
        self.eng_ops[eng].append(oid)
        return oid

    def mm(self, out, lhsT, rhs, start=True, stop=True, **kw):
        return self._rec("pe", lambda e: e.matmul(out.ap, lhsT.ap, rhs.ap, start=start, stop=stop, **kw),
                         [lhsT, rhs], [out])

    def tr(self, out, in_, ident):
        return self._rec("pe", lambda e: e.transpose(out.ap, in_.ap, ident.ap), [in_, ident], [out])

    def act(self, out, in_, func, bias=None, scale=None, accum=None, eng="act"):
        reads = [in_]
        kw = {}
        if bias is not None:
            if isinstance(bias, V):
                reads.append(bias)
                kw["bias"] = bias.ap
            else:
                kw["bias"] = bias
        if scale is not None:
            if isinstance(scale, V):
                reads.append(scale)
                kw["scale"] = scale.ap
            else:
                kw["scale"] = scale
        writes = [out]
        if accum is not None:
            writes.append(accum)
            kw["accum_out"] = accum.ap
        return self._rec("act", lambda e: e.activation(out.ap, in_.ap, func, **kw), reads, writes)

    def tt(self, out, in0, in1, op, eng="dve"):
        return self._rec(eng, lambda e: e.tensor_tensor(out.ap, in0.ap, in1.ap, op), [in0, in1], [out])

    def ts(self, out, in0, s1, s2, op0, op1=None, accum=None, eng="dve"):
        reads = [in0]
        a1 = s1.ap if isinstance(s1, V) else s1
        a2 = s2.ap if isinstance(s2, V) else s2
        if isinstance(s1, V):
            reads.append(s1)
        if isinstance(s2, V):
            reads.append(s2)
        writes = [out]
        kw = {}
        if op1 is not None:
            kw["op1"] = op1
        if accum is not None:
            kw["accum_out"] = accum.ap
            writes.append(accum)
        return self._rec(eng, lambda e: e.tensor_scalar(out.ap, in0.ap, a1, a2, op0, **kw), reads, writes)

    def stt(self, out, in0, scalar, in1, op0, op1, eng="dve"):
        reads = [in0, in1]
        a = scalar.ap if isinstance(scalar, V) else scalar
        if isinstance(scalar, V):
            reads.append(scalar)
        return self._rec(eng, lambda e: e.scalar_tensor_tensor(out.ap, in0.ap, a, in1.ap, op0, op1), reads, [out])

    def copy(self, out, in_, eng="dve"):
        if eng == "act":
            return self._rec("act", lambda e: e.copy(out.ap, in_.ap), [in_], [out])
        return self._rec(eng, lambda e: e.tensor_copy(out.ap, in_.ap), [in_], [out])

    def memset(self, out, val, eng="dve"):
        return self._rec(eng, lambda e: e.memset(out.ap, val), [], [out])

    def reduce(self, out, in_, op, axis=None, eng="dve"):
        axis = AX.X if axis is None else axis
        return self._rec(eng, lambda e: e.tensor_reduce(out.ap, in_.ap, axis, op), [in_], [out])

    def recip(self, out, in_):
        return self._rec("dve", lambda e: e.reciprocal(out.ap, in_.ap), [in_], [out])

    def scan(self, out, d0, d1, init, op0, op1):
        reads = [d0, d1]
        a = init.ap if isinstance(init, V) else init
        if isinstance(init, V):
            reads.append(init)
        return self._rec("dve", lambda e: e.tensor_tensor_scan(out.ap, d0.ap, d1.ap, a, op0, op1), reads, [out])

    def dma(self, out, in_, q="sp", **kw):
        return self._rec(q, lambda e: [e.dma_start(out=out.ap, in_=in_.ap, **kw)], [in_], [out], dma=True, ndma=1)

    def dmas(self, pairs, q="sp", **kw):
        outs = [p[0] for p in pairs]
        ins = [p[1] for p in pairs]
        return self._rec(q, lambda e: [e.dma_start(out=o.ap, in_=i.ap, **kw) for o, i in pairs], ins, outs,
                         dma=True, ndma=len(pairs))

    def emit(self):
        nc = self.nc
        ops = self.ops
        for op in ops:
            for d in op["deps"]:
                dop = ops[d]
                if dop["dma"]:
                    continue
                if dop["eng"] == op["eng"] and (op["eng"] not in SAME_ENG_SYNC):
                    continue
                dop["needed"] = True
        final_waits = []
        for key, oid in self.dma_last.items():
            final_waits.append((ops[oid]["sem"], ops[oid]["val"]))
        esems = {e: [] for e in self.ENGS}
        for e in self.ENGS:
            cnt = 0
            for oid in self.eng_ops[e]:
                op = ops[oid]
                if op["dma"] or not op["needed"]:
                    continue
                ep, val = cnt // EPOCH, cnt % EPOCH + 1
                if ep >= len(esems[e]):
                    esems[e].append(self.sem(f"c_{e}_{ep}"))
                op["sem"] = esems[e][ep]
                op["val"] = val
                op["cnt"] = cnt
                cnt += 1
        engobj = {"pe": nc.tensor, "act": nc.scalar, "dve": nc.vector, "pool": nc.gpsimd, "sp": nc.sync}
        self.nwaits = 0

        def emit_engine(ename, e):
            known_pos = {x: -1 for x in self.ENGS}
            known_dma = {}
            for oid in self.eng_ops[ename]:
                op = ops[oid]
                waits = {}
                for d in sorted(op["deps"]):
                    dop = ops[d]
                    if dop["dma"]:
                        s = dop["sem"]
                        if known_dma.get(id(s), 0) >= dop["val"]:
                            continue
                        known_dma[id(s)] = dop["val"]
                        waits[("d", id(s))] = (s, dop["val"])
                    else:
                        x = dop["eng"]
                        if x == ename and ename not in SAME_ENG_SYNC:
                            continue
                        if known_pos[x] >= dop["pos"]:
                            continue
                        known_pos[x] = dop["pos"]
                        waits[("c", x, dop["cnt"] // EPOCH)] = (dop["sem"], dop["val"])
                for s, v in waits.values():
                    e.wait_ge(s, v)
                    self.nwaits += 1
                r = op["fn"](e)
                if op["dma"]:
                    for ins in r:
                        ins.then_inc(op["sem"], 16)
                elif op["needed"]:
                    r.then_inc(op["sem"], 1)
            if ename == "sp":
                for s, v in final_waits:
                    e.wait_ge(s, v)

        with nc.Block() as block:
            @block.sync
            def _(e):
                emit_engine("sp", e)

            @block.tensor
            def _(e):
                emit_engine("pe", e)

            @block.scalar
            def _(e):
                emit_engine("act", e)

            @block.vector
            def _(e):
                emit_engine("dve", e)

            @block.gpsimd
            def _(e):
                emit_engine("pool", e)

import ml_dtypes

NPBF = ml_dtypes.bfloat16
NCORES = 8
D = 2048
SEQ = 16384
CTX = 256
TL = SEQ // NCORES
NTOK = TL + CTX
INW = 3776
DFF = 5632
LN_EPS = 1e-5
RMS_EPS = 1e-6
ALPHA = 4.0 ** 0.25


def new_prog():
    nc = bass.Bass("TRN2", target_bir_lowering=False)
    return nc, Prog(nc)


def bc1(v, n, w):
    return V(v.ap.unsqueeze(1).to_broadcast([128, n, w]), v.bufs)


class Rot:
    def __init__(self, items):
        self.items = items
        self.i = 0

    def next(self):
        x = self.items[self.i % len(self.items)]
        self.i += 1
        return x


WTOT = D * D + 3 * D * DFF
WCF = WTOT // NCORES // 128


def build_Z():
    nc, P = new_prog()
    cT = P.dram("cT", [128, 16, 2], F32, "ExternalInput")
    w = P.dram("w", [2, 2048, 1536], F32, "ExternalInput")
    b = P.dram("b", [2, 2, 1536], F32, "ExternalInput")
    out = P.dram("out", [2, 2, 1536], F32, "ExternalOutput")
    wc_d = P.dram("wc", [128, 2 * WCF], F32, "ExternalInput")
    wcb_d = P.dram("wcb", [128, 2 * WCF], BF16, "ExternalOutput")
    cb_r = Rot([P.sb(f"wcs{i}", [128, 4096], BF16) for i in range(2)])
    for c0 in range(0, 2 * WCF, 4096):
        c1 = min(2 * WCF, c0 + 4096)
        t = cb_r.next()
        P.dma(t[:, 0:c1 - c0], wc_d[:, c0:c1], q="pool")
        P.dma(wcb_d[:, c0:c1], t[:, 0:c1 - c0], q="act")
    cs = P.sb("cs", [128, 16, 2], F32)
    sc = P.sb("sc", [128, 16, 2], F32)
    sh = P.sb("sh", [128, 16, 2], BF16)
    sd = P.sb("sd", [128, 16, 2], F32)
    sl = P.sb("sl", [128, 16, 2], BF16)
    bs = P.sb("bs", [2, 2, 1536], F32)
    os_ = P.sb("os", [2, 2, 1536], F32)
    P.dma(cs, cT)
    P.dma(bs, b)
    P.act(sc, cs, AF.Silu)
    P.copy(sh, sc)
    P.tt(sd, sc, sh, ALU.subtract)
    P.copy(sl, sd)
    wt = Rot([P.sb(f"wt{i}", [128, 16, 512], F32) for i in range(2)])
    wh = Rot([P.sb(f"wh{i}", [128, 16, 512], BF16) for i in range(2)])
    wd = Rot([P.sb(f"wd{i}", [128, 16, 512], F32) for i in range(1)])
    wl = Rot([P.sb(f"wl{i}", [128, 16, 512], BF16) for i in range(2)])
    ps = Rot([P.ps(f"ps{i}", [2, 512], F32) for i in range(2)])
    for l in range(2):
        for cc in range(3):
            t = wt.next()
            h = wh.next()
            d_ = wd.next()
            lo = wl.next()
            p = ps.next()
            P.dma(t, w[l, :, cc * 512:(cc + 1) * 512].re("(kc p) n -> p kc n", p=128))
            P.copy(h, t, eng="act")
            P.tt(d_, t, h, ALU.subtract)
            P.copy(lo, d_, eng="pool")
            n = 0
            for kc in range(16):
                for (a, b_) in ((sh, h), (sh, lo), (sl, h)):
                    P.mm(p, a[:, kc, :], b_[:, kc, :], start=(n == 0), stop=(n == 47))
                    n += 1
            P.tt(os_[:, l, cc * 512:(cc + 1) * 512], p, bs[:, l, cc * 512:(cc + 1) * 512], ALU.add)
    P.dma(out, os_)
    P.emit()
    return nc


def _c_weight_flat(inp, l):
    nff = DFF // 128
    wg = inp["ffn_wg"][l].reshape(16, 128, nff, 128).transpose(2, 1, 0, 3)
    wu = inp["ffn_wu"][l].reshape(16, 128, nff, 128).transpose(2, 1, 0, 3)
    return np.concatenate([inp["w_out"][l].ravel(), wg.ravel(), wu.ravel(), inp["ffn_wd"][l].ravel()])


def run_Z(inp):
    flats = [_c_weight_flat(inp, l) for l in range(2)]
    part = WTOT // NCORES
    cT = np.stack([inp["c"][0], inp["c_ctx"]], axis=-1).reshape(16, 128, 2).transpose(1, 0, 2)
    cT = np.ascontiguousarray(cT, dtype=np.float32)
    nc = build_Z()
    maps = []
    for c in range(NCORES):
        cols = slice(c * 1536, (c + 1) * 1536)
        maps.append({
            "cT": cT,
            "w": np.ascontiguousarray(inp["w_ada"][:, :, cols]),
            "b": np.ascontiguousarray(np.broadcast_to(inp["b_ada"][None, :, cols], (2, 2, 1536))),
            "wc": np.ascontiguousarray(np.concatenate(
                [flats[l][c * part:(c + 1) * part].reshape(128, WCF) for l in range(2)], axis=1)),
        })
    res = run_bass_kernel_spmd(nc, maps, core_ids=list(range(NCORES)))
    m = np.concatenate([r["out"] for r in res.results], axis=-1)
    wb = []
    for l in range(2):
        flat = np.concatenate([r["wcb"][:, l * WCF:(l + 1) * WCF].reshape(-1) for r in res.results])
        o = 0
        d = {}
        for name, shape in (("wout", (D, D)), ("wg", (DFF // 128, 128, 2048)), ("wu", (DFF // 128, 128, 2048)),
                            ("wd", (DFF, D))):
            n = int(np.prod(shape))
            d[name] = np.ascontiguousarray(flat[o:o + n].reshape(shape))
            o += n
        wb.append(d)
    return m.reshape(2, 2, 6, D), wb


def build_A(stage=9):
    nc, P = new_prog()
    xin = P.dram("xin", [NTOK, D], F32, "ExternalInput")
    w_in = P.dram("w_in", [D, INW], F32, "ExternalInput")
    modc = P.dram("modc", [128, 2, 2, 16], F32, "ExternalInput")
    idf_d = P.dram("idf", [128, 128], F32, "ExternalInput")
    idb_d = P.dram("idb", [128, 128], BF16, "ExternalInput")
    cs_d = P.dram("cs", [128, NTOK // 128, 64], F32, "ExternalInput")
    gq_d = P.dram("gq", [128, 384], F32, "ExternalInput")
    gkv_d = P.dram("gkv", [128, 256], F32, "ExternalInput")
    wuq_d = P.dram("wuq", [384, 768], F32, "ExternalInput")
    wukv_d = P.dram("wukv", [256, 1024], F32, "ExternalInput")
    dftc_d = P.dram("dftc", [128, 256], BF16, "ExternalInput")
    o_qk = P.dram("o_qk", [NTOK, 1024], BF16, "ExternalOutput")
    o_v = P.dram("o_v", [NTOK, 512], BF16, "ExternalOutput")
    o_lr = P.dram("o_lr", [NTOK, 512], F32, "ExternalOutput")
    o_gg = P.dram("o_gg", [NTOK, 512], F32, "ExternalOutput")
    o_q = P.dram("o_q", [NTOK, 768], BF16, "ExternalOutput")
    o_kv = P.dram("o_kv", [NTOK, 1024], BF16, "ExternalOutput")
    o_kr = P.dram("o_kr", [NTOK, 64], BF16, "ExternalOutput")
    o_z = P.dram("o_z", [NTOK, 4, 256], BF16, "ExternalOutput")

    idf = P.sb("idf_s", [128, 128], F32)
    idb = P.sb("idb_s", [128, 128], BF16)
    mc = P.sb("mc", [128, 2, 2, 16], F32)
    cs = P.sb("cs_s", [128, NTOK // 128, 64], F32)
    gq = P.sb("gq_s", [128, 384], F32)
    gkv = P.sb("gkv_s", [128, 256], F32)
    dftc = P.sb("dftc_s", [128, 256], BF16)
    wuq = P.sb("wuq_s", [128, 3, 768], BF16)
    wukv = P.sb("wukv_s", [128, 2, 1024], BF16)
    P.dma(idf, idf_d)
    P.dma(idb, idb_d)
    P.dma(mc, modc)
    P.dma(cs, cs_d)
    P.dma(gq, gq_d)
    P.dma(gkv, gkv_d)
    P.dma(dftc, dftc_d)
    P.ts(mc[:, :, 0, :], mc[:, :, 0, :], 1.0, None, ALU.add)
    win = [P.sb(f"win{kc}", [128, INW], BF16) for kc in range(16)]
    for kc in range(16):
        P.dma(win[kc], w_in[kc * 128:(kc + 1) * 128, :], q="pool")
    P.dma(wuq, wuq_d.re("(kc p) n -> p kc n", p=128), q="pool")
    P.dma(wukv, wukv_d.re("(kc p) n -> p kc n", p=128), q="pool")

    xt_r = Rot([P.sb(f"xt{i}", [128, D], F32) for i in range(2)])
    xT_r = Rot([P.sb(f"xT{i}", [128, 16, 256], BF16) for i in range(2)])
    pTx = Rot([P.ps(f"pTx{i}", [128, 4, 128], BF16) for i in range(2)])
    xb_r = Rot([P.sb(f"xb{i}", [128, D], BF16) for i in range(1)])
    pU = Rot([P.ps(f"pU{i}", [128, 512], F32) for i in range(3)])
    pTb = Rot([P.ps(f"pTb{i}", [128, 4, 128], BF16) for i in range(1)])
    pM = Rot([P.ps(f"pM{i}", [128, 512], F32) for i in range(2)])
    tmp = Rot([P.sb(f"tmp{i}", [128, 256], F32) for i in range(4)])
    qk_r = Rot([P.sb(f"qk{i}", [128, 1024], BF16) for i in range(2)])
    v_r = Rot([P.sb(f"vb{i}", [128, 512], BF16) for i in range(2)])
    f_r = Rot([P.sb(f"fs{i}", [128, 512], F32) for i in range(2)])
    cq_r = Rot([P.sb(f"cq{i}", [128, 384], F32) for i in range(1)])
    cqn_r = Rot([P.sb(f"cqn{i}", [128, 384], BF16) for i in range(2)])
    cT_r = Rot([P.sb(f"cTt{i}", [128, 3, 128], BF16) for i in range(2)])
    st_r = Rot([P.sb(f"st{i}", [128, 4], F32) for i in range(4)])
    qs_r = Rot([P.sb(f"qs{i}", [128, 768], F32) for i in range(1)])
    qb_r = Rot([P.sb(f"qb{i}", [128, 768], BF16) for i in range(2)])
    kv_r = Rot([P.sb(f"kvb{i}", [128, 1024], BF16) for i in range(2)])
    kr_r = Rot([P.sb(f"krb{i}", [128, 64], BF16) for i in range(2)])
    gT_r = Rot([P.sb(f"gT{i}", [128, 4, 128], BF16) for i in range(2)])
    z_r = Rot([P.sb(f"zb{i}", [128, 4, 256], BF16) for i in range(2)])

    def rope(x1, x2, o1, o2, cos_b, sin_b, n):
        def tv():
            t = tmp.next()
            if n == 1:
                return t[:, 0:32]
            return t[:, 0:n * 32].re("p (n w) -> p n w", w=32)
        a, b_, c, d = tv(), tv(), tv(), tv()
        P.tt(a, x1, cos_b, ALU.mult)
        P.tt(b_, x2, sin_b, ALU.mult)
        P.tt(c, x1, sin_b, ALU.mult)
        P.tt(d, x2, cos_b, ALU.mult)
        P.tt(o1, a, b_, ALU.subtract)
        P.tt(o2, c, d, ALU.add)

    def rmsnorm_T(src_ps, n, g, nk, extra_ps=None, extra=0):
        raw = cq_r.next()
        st = st_r.next()
        P.act(raw[:, 0:n], src_ps, AF.Identity)
        if extra_ps is not None:
            P.act(raw[:, n:n + extra], extra_ps, AF.Identity)
        sq = f_r.next()
        P.act(sq[:, 0:n], src_ps, AF.Square, accum=st[:, 0:1])
        P.ts(st[:, 1:2], st[:, 0:1], 1.0 / n, RMS_EPS, ALU.mult, ALU.add)
        P.act(st[:, 2:3], st[:, 1:2], AF.Sqrt)
        P.recip(st[:, 3:4], st[:, 2:3])
        nb = cqn_r.next()
        P.stt(nb[:, 0:n], raw[:, 0:n], st[:, 3:4], g, ALU.mult, ALU.mult)
        pt = pTb.next()
        for k in range(nk):
            P.tr(pt[:, k, :], nb[:, k * 128:(k + 1) * 128], idb)
        cT = cT_r.next()
        P.copy(cT[:, 0:nk, :], pt[:, 0:nk, :], eng="act")
        return cT, raw

    groups = [(g * 256, 256, 0) for g in range(TL // 256)] + [(TL, 256, 1)]
    for (tok0, G, j) in groups:
        nt = G // 128
        xT = xT_r.next()
        for ti in range(nt):
            xt = xt_r.next()
            P.dma(xt, xin[tok0 + ti * 128: tok0 + (ti + 1) * 128, :])
            xb = xb_r.next()
            P.copy(xb, xt, eng="pool")
            for q4 in range(4):
                pt = pTx.next()
                for i in range(4):
                    kc = q4 * 4 + i
                    P.tr(pt[:, i, :], xb[:, kc * 128:(kc + 1) * 128], idb)
                for i in range(4):
                    kc = q4 * 4 + i
                    P.act(xT[:, kc, ti * 128:(ti + 1) * 128], pt[:, i, :], AF.Identity,
                          scale=mc[:, j, 0, kc:kc + 1], bias=mc[:, j, 1, kc:kc + 1])
        for ti in range(nt):
            r0 = tok0 + ti * 128
            tile_idx = r0 // 128
            cos_t = cs[:, tile_idx, 0:32]
            sin_t = cs[:, tile_idx, 32:64]
            lhs = lambda kc: xT[:, kc, ti * 128:(ti + 1) * 128]

            def proj(c0, n, pool=pU):
                pu = pool.next()
                for kc in range(16):
                    P.mm(pu[:, 0:n], lhs(kc), win[kc][:, c0:c0 + n], start=(kc == 0), stop=(kc == 15))
                return pu

            qk = qk_r.next()
            for s in (range(2) if stage >= 2 else []):
                pu = proj(s * 512, 512)
                src = pu.re("p (n w) -> p n w", w=64)
                dst = qk[:, s * 512:(s + 1) * 512].re("p (n w) -> p n w", w=64)
                rope(src[:, :, 0:32], src[:, :, 32:64], dst[:, :, 0:32], dst[:, :, 32:64],
                     bc1(cos_t, 8, 32), bc1(sin_t, 8, 32), 8)
            if stage >= 2:
                P.dma(o_qk[r0:r0 + 128, :], qk)
            pu = proj(1024, 512)
            vb = v_r.next()
            P.copy(vb, pu, eng="act")
            P.dma(o_v[r0:r0 + 128, :], vb)
            if stage < 3:
                continue
            pu = proj(1536, 512)
            fs = f_r.next()
            P.copy(fs, pu, eng="act")
            P.dma(o_lr[r0:r0 + 128, :], fs)
            pu = proj(2048, 512)
            fs = f_r.next()
            P.act(fs, pu, AF.Gelu)
            P.dma(o_gg[r0:r0 + 128, :], fs)
            if stage < 3.3:
                continue
            pu = proj(2560, 384)
            cqT, _raw = rmsnorm_T(pu[:, 0:384], 384, gq, 3)
            if stage < 3.6:
                continue
            qs = qs_r.next()
            for (c0, n) in ((0, 512), (512, 256)):
                pm = pM.next()
                for k in range(3):
                    P.mm(pm[:, 0:n], cqT[:, k, :], wuq[:, k, c0:c0 + n], start=(k == 0), stop=(k == 2))
                P.copy(qs[:, c0:c0 + n], pm[:, 0:n], eng="act")
            qb = qb_r.next()
            P.copy(qb, qs, eng="pool")
            qs3 = qs.re("p (h w) -> p h w", w=192)
            qb3 = qb.re("p (h w) -> p h w", w=192)
            if stage >= 3.8:
              rope(qs3[:, :, 128:160], qs3[:, :, 160:192], qb3[:, :, 128:160], qb3[:, :, 160:192],
                 bc1(cos_t, 4, 32), bc1(sin_t, 4, 32), 4)
            P.dma(o_q[r0:r0 + 128, :], qb)
            if stage < 4:
                continue
            pu = proj(2944, 320)
            ckvT, raw = rmsnorm_T(pu[:, 0:256], 256, gkv, 2, extra_ps=pu[:, 256:320], extra=64)
            krb = kr_r.next()
            rope(raw[:, 256:288], raw[:, 288:320], krb[:, 0:32], krb[:, 32:64], cos_t, sin_t, 1)
            P.dma(o_kr[r0:r0 + 128, :], krb)
            kvb = kv_r.next()
            for c0 in (0, 512):
                pm = pM.next()
                for k in range(2):
                    P.mm(pm, ckvT[:, k, :], wukv[:, k, c0:c0 + 512], start=(k == 0), stop=(k == 1))
                P.copy(kvb[:, c0:c0 + 512], pm, eng="act")
            P.dma(o_kv[r0:r0 + 128, :], kvb)
        if stage < 5:
            continue
        for ti in range(nt):
            r0 = tok0 + ti * 128
            lhs = lambda kc: xT[:, kc, ti * 128:(ti + 1) * 128]
            pu = pU.next()
            for kc in range(16):
                P.mm(pu, lhs(kc), win[kc][:, 3264:3776], start=(kc == 0), stop=(kc == 15))
            ub = v_r.next()
            P.copy(ub, pu, eng="act")
            pt = pTb.next()
            for g in range(4):
                P.tr(pt[:, g, :], ub[:, g * 128:(g + 1) * 128], idb)
            gT = gT_r.next()
            P.copy(gT, pt, eng="act")
            zb = z_r.next()
            for g in range(4):
                pm = pM.next()
                P.mm(pm[:, 0:256], gT[:, g, :], dftc, start=True, stop=True)
                P.copy(zb[:, g, :], pm[:, 0:256])
            P.dma(o_z[r0:r0 + 128, :, :], zb)
    P.emit()
    return nc


def rope_tables():
    n = np.arange(SEQ)
    row = (n // 64).astype(np.float32)
    col = (n % 64).astype(np.float32)
    inv = (10000.0 ** (-np.arange(0, 32, 2, dtype=np.float32) / np.float32(32))).astype(np.float32)
    ang = np.concatenate([row[:, None] * inv, col[:, None] * inv], axis=-1).astype(np.float32)
    return np.cos(ang).astype(np.float32), np.sin(ang).astype(np.float32)


def col_layout(v):
    return np.ascontiguousarray(v.reshape(16, 128).T)


_CONST = {}


def consts():
    if not _CONST:
        cos, sin = rope_tables()
        _CONST["cos"], _CONST["sin"] = cos, sin
        _CONST["idf"] = np.eye(128, dtype=np.float32)
        _CONST["idb"] = np.eye(128).astype(NPBF)
        k = np.arange(128)
        th = 2.0 * np.pi * np.outer(k, k) / 128.0
        C, S = np.cos(th), np.sin(th)
        _CONST["C128"], _CONST["S128"] = C, S
        _CONST["dftc"] = np.concatenate([C, -S], axis=1).astype(NPBF)
    return _CONST


def run_A(l, x_lat, x_ctx, m, inp, stage=9):
    K = consts()
    nc = build_A(stage)
    modc = np.zeros((128, 2, 2, 16), np.float32)
    for j in range(2):
        modc[:, j, 0, :] = col_layout(m[j, l, 1])
        modc[:, j, 1, :] = col_layout(m[j, l, 0])
    gq = np.ascontiguousarray(np.broadcast_to(inp["mla_qn_g"][l][None, :], (128, 384)))
    gkv = np.ascontiguousarray(np.broadcast_to(inp["mla_kvn_g"][l][None, :], (128, 256)))
    maps = []
    for c in range(NCORES):
        cs = np.zeros((NTOK, 64), np.float32)
        cs[:TL, 0:32] = K["cos"][c * TL:(c + 1) * TL]
        cs[:TL, 32:64] = K["sin"][c * TL:(c + 1) * TL]
        cs[TL:, 0:32] = 1.0
        cs = np.ascontiguousarray(cs.reshape(NTOK // 128, 128, 64).transpose(1, 0, 2))
        maps.append({
            "xin": np.ascontiguousarray(np.concatenate([x_lat[c * TL:(c + 1) * TL], x_ctx], axis=0)),
            "w_in": np.ascontiguousarray(inp["w_in"][l]),
            "modc": modc, "idf": K["idf"], "idb": K["idb"], "cs": cs, "gq": gq, "gkv": gkv,
            "wuq": np.ascontiguousarray(inp["mla_wuq"][l]), "wukv": np.ascontiguousarray(inp["mla_wukv"][l]),
            "dftc": K["dftc"],
        })
    res = run_bass_kernel_spmd(nc, maps, core_ids=list(range(NCORES)))
    out = {}
    for k in ("o_qk", "o_v", "o_lr", "o_gg", "o_q", "o_kv", "o_kr", "o_z"):
        lat = np.concatenate([r[k][:TL] for r in res.results], axis=0)
        ctx = res.results[0][k][TL:]
        out[k] = np.concatenate([ctx, lat], axis=0)
    return out


NK = CTX + SEQ
NKT = NK // 128
NQ = TL + CTX


def build_B1():
    nc, P = new_prog()
    qda_d = P.dram("qda", [4, 128, NQ], BF16, "ExternalInput")
    kda_d = P.dram("kda", [4, 128, NK], BF16, "ExternalInput")
    vda_d = P.dram("vda", [4, 128, NKT, 129], BF16, "ExternalInput")
    qn_d = P.dram("qn", [4, 128, NQ], BF16, "ExternalInput")
    qr_d = P.dram("qr", [4, 64, NQ], BF16, "ExternalInput")
    kn_d = P.dram("kn", [4, 128, NK], BF16, "ExternalInput")
    kr_d = P.dram("kr", [64, NK], BF16, "ExternalInput")
    vm_d = P.dram("vm", [4, 128, NKT, 129], BF16, "ExternalInput")
    lqk_d = P.dram("lqk", [128, 4, 64], F32, "ExternalInput")
    gsub_d = P.dram("gsub", [128, 128], F32, "ExternalInput")
    li_d = P.dram("li", [128, 2], F32, "ExternalInput")
    o_da = P.dram("o_da", [NQ, 512], F32, "ExternalOutput")
    o_mla = P.dram("o_mla", [NQ, 512], F32, "ExternalOutput")

    lqk = P.sb("lqk_s", [128, 4, 64], F32)
    gsub = P.sb("gsub_s", [128, 128], F32)
    li = P.sb("li_s", [128, 2], F32)
    P.dma(lqk, lqk_d)
    P.dma(gsub, gsub_d)
    P.dma(li, li_d)
    pr = P.sb("pr", [128, 2, 64], F32)
    sm = P.sb("sm", [128, 8], F32)
    P.tt(pr[:, 0, :], lqk[:, 0, :], lqk[:, 1, :], ALU.mult)
    P.tt(pr[:, 1, :], lqk[:, 2, :], lqk[:, 3, :], ALU.mult)
    P.reduce(sm[:, 0:1], pr[:, 0, :], ALU.add)
    P.reduce(sm[:, 1:2], pr[:, 1, :], ALU.add)
    P.act(sm[:, 2:4], sm[:, 0:2], AF.Exp)
    P.tt(sm[:, 4:5], sm[:, 2:3], sm[:, 3:4], ALU.subtract)
    P.tt(sm[:, 5:6], sm[:, 4:5], li[:, 0:1], ALU.add)
    P.ts(sm[:, 6:7], sm[:, 5:6], -1.0, None, ALU.mult)
    nlam = sm[:, 6:7]
    gl = P.sb("gl", [128, 128], F32)
    P.ts(gl, gsub, li[:, 1:2], None, ALU.mult)

    kT_r = Rot([P.sb(f"kT{i}", [128, NK], BF16) for i in range(2)])
    v_r = Rot([P.sb(f"vv{i}", [128, NKT, 129], BF16) for i in range(2)])
    krs = P.sb("krs", [64, NK], BF16)
    qa_r = Rot([P.sb(f"qa{i}", [128, NQ], BF16) for i in range(2)])
    qb_r = Rot([P.sb(f"qbb{i}", [64, NQ], BF16) for i in range(2)])
    pt_r = Rot([P.sb(f"pt{i}", [128, 512], BF16) for i in range(3)])
    pS = Rot([P.ps(f"pS{i}", [128, 512], F32) for i in range(4)])
    pO = [P.ps(f"pO{i}", [128, 129], F32) for i in range(4)]
    n0 = P.sb("n0", [128, NQ // 128, 128], F32)
    os_r = Rot([P.sb(f"os{i}", [128, 4, 128], F32) for i in range(2)])
    st_r = Rot([P.sb(f"stt{i}", [128, 8], F32) for i in range(4)])
    d_r = Rot([P.sb(f"dd{i}", [128, 128], F32) for i in range(3)])
    junk = P.sb("junk", [128, 128], F32)
    P.dma(krs, kr_d)

    qchunks = [(i * 512, 512, list(range(NKT))) for i in range(TL // 512)] + [(TL, CTX, [0, 1])]

    def attend(terms, vt, scale, fin, out_d, col0):
        LOOK = 2
        for (q0, nq, ktiles) in qchunks:
            nsub = nq // 128
            nkt = len(ktiles)

            def score(t):
                s = pS.next()
                for i, (kT, qT) in enumerate(terms):
                    P.mm(s[:, 0:nq], kT[:, t * 128:(t + 1) * 128], qT[:, q0:q0 + nq],
                         start=(i == 0), stop=(i == len(terms) - 1))
                return s

            def consume(ii, t, s):
                pt = pt_r.next()
                P.act(pt[:, 0:nq], s[:, 0:nq], AF.Exp, scale=scale)
                for sub in range(nsub):
                    P.mm(pO[sub], pt[:, sub * 128:(sub + 1) * 128], vt[:, t, :],
                         start=(ii == 0), stop=(ii == nkt - 1))

            pend = []
            for ii, t in enumerate(ktiles):
                pend.append((ii, t, score(t)))
                if len(pend) > LOOK:
                    consume(*pend.pop(0))
            while pend:
                consume(*pend.pop(0))
            ost = os_r.next() if out_d is not None else None
            for sub in range(nsub):
                fin(q0 // 128 + sub, pO[sub], None if ost is None else ost[:, sub, :])
            if out_d is not None:
                P.dma(out_d[q0:q0 + nq, col0:col0 + 128].re("(s p) c -> p s c", p=128), ost[:, 0:nsub, :])

    def fin_plain(gi, po, dst):
        st = st_r.next()
        P.recip(st[:, 0:1], po[:, 128:129])
        P.ts(dst, po[:, 0:128], st[:, 0:1], None, ALU.mult)

    def fin_c0(gi, po, dst):
        st = st_r.next()
        P.recip(st[:, 0:1], po[:, 128:129])
        P.ts(n0[:, gi, :], po[:, 0:128], st[:, 0:1], None, ALU.mult)

    def fin_c1(gi, po, dst):
        st = st_r.next()
        P.recip(st[:, 0:1], po[:, 128:129])
        t1 = d_r.next()
        P.ts(t1, po[:, 0:128], st[:, 0:1], None, ALU.mult)
        d = d_r.next()
        P.stt(d, t1, nlam, n0[:, gi, :], ALU.mult, ALU.add)
        P.act(junk, d, AF.Square, accum=st[:, 1:2])
        P.ts(st[:, 2:3], st[:, 1:2], 1.0 / 128, RMS_EPS, ALU.mult, ALU.add)
        P.act(st[:, 3:4], st[:, 2:3], AF.Sqrt)
        P.recip(st[:, 4:5], st[:, 3:4])
        P.stt(dst, d, st[:, 4:5], gl, ALU.mult, ALU.mult)

    for h in range(4):
        kT = kT_r.next()
        vt = v_r.next()
        qa = qa_r.next()
        P.dma(kT, kda_d[h])
        P.dma(vt, vda_d[h], q="act")
        P.dma(qa, qda_d[h])
        attend([(kT[0:64, :], qa[0:64, :])], vt, 0.125, fin_c0, None, 0)
        attend([(kT[64:128, :], qa[64:128, :])], vt, 0.125, fin_c1, o_da, h * 128)
    for h in range(4):
        kT = kT_r.next()
        vt = v_r.next()
        qa = qa_r.next()
        qb = qb_r.next()
        P.dma(kT, kn_d[h])
        P.dma(vt, vm_d[h], q="act")
        P.dma(qa, qn_d[h])
        P.dma(qb, qr_d[h])
        attend([(kT, qa), (krs, qb)], vt, 192.0 ** -0.5, fin_plain, o_mla, h * 128)
    P.emit()
    return nc


def _aug_v(v):
    o = np.ones((128, NKT, 129), NPBF)
    o[:, :, 0:128] = v.reshape(NKT, 128, 128).transpose(1, 0, 2)
    return o


def lambda_init(l):
    import math
    return 0.8 - 0.6 * math.exp(-0.3 * l)


def run_B1(l, A, inp):
    nc = build_B1()
    qk = A["o_qk"].reshape(NK, 2, 4, 128)
    kda = np.ascontiguousarray(qk[:, 1].transpose(1, 2, 0))
    vda = np.stack([_aug_v(A["o_v"][:, h * 128:(h + 1) * 128]) for h in range(4)])
    q = A["o_q"].reshape(NK, 4, 192)
    kv = A["o_kv"].reshape(NK, 4, 256)
    kn = np.ascontiguousarray(kv[:, :, 0:128].transpose(1, 2, 0))
    vm = np.stack([_aug_v(kv[:, h, 128:256]) for h in range(4)])
    kr = np.ascontiguousarray(A["o_kr"].T)
    lqk = np.stack([inp["da_lq1"][l], inp["da_lk1"][l], inp["da_lq2"][l], inp["da_lk2"][l]])
    lqk = np.ascontiguousarray(np.broadcast_to(lqk[None], (128, 4, 64)), dtype=np.float32)
    gsub = np.ascontiguousarray(np.broadcast_to(inp["da_subln_g"][l][None], (128, 128)), dtype=np.float32)
    li = np.zeros((128, 2), np.float32)
    li[:, 0] = lambda_init(l)
    li[:, 1] = 1.0 - lambda_init(l)
    maps = []
    for c in range(NCORES):
        rows = np.concatenate([np.arange(CTX + c * TL, CTX + (c + 1) * TL), np.arange(CTX)])
        maps.append({
            "qda": np.ascontiguousarray(qk[rows, 0].transpose(1, 2, 0)),
            "kda": kda, "vda": vda,
            "qn": np.ascontiguousarray(q[rows][:, :, 0:128].transpose(1, 2, 0)),
            "qr": np.ascontiguousarray(q[rows][:, :, 128:192].transpose(1, 2, 0)),
            "kn": kn, "kr": kr, "vm": vm, "lqk": lqk, "gsub": gsub, "li": li,
        })
    res = run_bass_kernel_spmd(nc, maps, core_ids=list(range(NCORES)))
    out = {}
    for k in ("o_da", "o_mla"):
        lat = np.concatenate([r[k][:TL] for r in res.results], axis=0)
        out[k] = np.concatenate([res.results[0][k][TL:], lat], axis=0)
    return out


LCH = 1024


def build_B2a():
    nc, P = new_prog()
    xblk_d = P.dram("xblk", [128, NK], F32, "ExternalInput")
    xdup_d = P.dram("xdup", [128, NK], F32, "ExternalInput")
    cwb_d = P.dram("cwb", [128, 5], F32, "ExternalInput")
    cwd_d = P.dram("cwd", [128, 5], F32, "ExternalInput")
    wr_d = P.dram("wr", [128, 128], F32, "ExternalInput")
    wi_d = P.dram("wi", [128, 128], F32, "ExternalInput")
    pp_d = P.dram("pp", [128, 4], F32, "ExternalInput")
    h_d = P.dram("h", [128, NK], F32, "ExternalOutput")

    cwb = P.sb("cwb_s", [128, 5], F32)
    cwd = P.sb("cwd_s", [128, 5], F32)
    pp = P.sb("pp_s", [128, 4], F32)
    wr = P.sb("wr_s", [128, 128], BF16)
    wi = P.sb("wi_s", [128, 128], BF16)
    P.dma(cwb, cwb_d)
    P.dma(cwd, cwd_d)
    P.dma(pp, pp_d)
    P.dma(wr, wr_d, q="pool")
    P.dma(wi, wi_d, q="pool")
    sp = P.sb("sp", [128, 4], F32)
    P.act(sp[:, 0:1], pp[:, 2:3], AF.Exp, scale=-1.0)
    P.act(sp[:, 1:2], sp[:, 0:1], AF.Ln, bias=1.0)
    P.ts(sp[:, 2:3], sp[:, 1:2], -8.0, None, ALU.mult)
    P.ts(sp[:, 3:4], sp[:, 1:2], -16.0, None, ALU.mult)
    Aa = P.sb("Aa", [128, NK], F32)
    Uu = P.sb("Uu", [128, NK], F32)
    segs = [(0, CTX)] + [(CTX + i * LCH, LCH) for i in range(SEQ // LCH)]
    seg_end = {CTX, NK}
    seg_start = {0, CTX}
    xb_r = Rot([P.sb(f"xbk{i}", [128, LCH + 3], F32) for i in range(2)])
    xd_r = Rot([P.sb(f"xdp{i}", [128, LCH + 3], F32) for i in range(2)])
    cb_r = Rot([P.sb(f"cb{i}", [128, LCH], F32) for i in range(1)])
    cbb_r = Rot([P.sb(f"cbb{i}", [128, LCH], BF16) for i in range(2)])
    cd_r = Rot([P.sb(f"cd{i}", [128, LCH], F32) for i in range(2)])
    r_r = Rot([P.sb(f"rr{i}", [128, LCH], F32) for i in range(2)])
    i_r = Rot([P.sb(f"ii{i}", [128, LCH], F32) for i in range(2)])
    t_r = Rot([P.sb(f"tt{i}", [128, LCH], F32) for i in range(3)])
    pz = Rot([P.ps(f"pz{i}", [128, 512], F32) for i in range(4)])

    def conv(xt, cw, out, n):
        P.ts(out[:, 0:n], xt[:, 0:n], cw[:, 0:1], cw[:, 4:5], ALU.mult, ALU.add)
        for k in (1, 2, 3):
            P.stt(out[:, 0:n], xt[:, k:k + n], cw[:, k:k + 1], out[:, 0:n], ALU.mult, ALU.add)

    def load_halo(xt, src, t0, n):
        lo = 2 if t0 not in seg_start else 0
        hi = 1 if (t0 + n) not in seg_end else 0
        if lo == 0:
            P.memset(xt[:, 0:2], 0.0)
        if hi == 0:
            P.memset(xt[:, n + 2:n + 3], 0.0)
        P.dma(xt[:, 2 - lo:n + 2 + hi], src[:, t0 - lo:t0 + n + hi])

    for (t0, n) in segs:
        xb = xb_r.next()
        xd = xd_r.next()
        load_halo(xb, xblk_d, t0, n)
        load_halo(xd, xdup_d, t0, n)
        cb = cb_r.next()
        conv(xb, cwb, cb, n)
        cbb = cbb_r.next()
        P.copy(cbb[:, 0:n], cb[:, 0:n], eng="pool")
        cd = cd_r.next()
        conv(xd, cwd, cd, n)
        r = r_r.next()
        ig = i_r.next()
        for c0 in range(0, n, 512):
            w = min(512, n - c0)
            p1 = pz.next()
            P.mm(p1[:, 0:w], wr, cbb[:, c0:c0 + w], start=True, stop=True)
            P.act(r[:, c0:c0 + w], p1[:, 0:w], AF.Sigmoid, bias=pp[:, 0:1])
            p2 = pz.next()
            P.mm(p2[:, 0:w], wi, cbb[:, c0:c0 + w], start=True, stop=True)
            P.act(ig[:, c0:c0 + w], p2[:, 0:w], AF.Sigmoid, bias=pp[:, 1:2])
        P.act(Aa[:, t0:t0 + n], r[:, 0:n], AF.Exp, scale=sp[:, 2:3])
        x2 = t_r.next()
        q = t_r.next()
        P.ts(x2[:, 0:n], r[:, 0:n], sp[:, 3:4], None, ALU.mult)
        P.ts(q[:, 0:n], x2[:, 0:n], 0.25, 1.0, ALU.mult, ALU.add)
        P.stt(q[:, 0:n], q[:, 0:n], 1.0 / 3.0, x2[:, 0:n], ALU.mult, ALU.mult)
        P.ts(q[:, 0:n], q[:, 0:n], 1.0, None, ALU.add)
        P.stt(q[:, 0:n], q[:, 0:n], 0.5, x2[:, 0:n], ALU.mult, ALU.mult)
        P.ts(q[:, 0:n], q[:, 0:n], 1.0, None, ALU.add)
        P.stt(q[:, 0:n], q[:, 0:n], -1.0, x2[:, 0:n], ALU.mult, ALU.mult)
        cf = t_r.next()
        P.act(cf[:, 0:n], q[:, 0:n], AF.Sqrt)
        P.tt(ig[:, 0:n], ig[:, 0:n], cd[:, 0:n], ALU.mult)
        P.tt(Uu[:, t0:t0 + n], ig[:, 0:n], cf[:, 0:n], ALU.mult)

    hf_r = Rot([P.sb(f"hf{i}", [128, LCH], F32) for i in range(2)])
    hb_r = Rot([P.sb(f"hb{i}", [128, LCH], F32) for i in range(2)])
    prev = None
    for (t0, n) in segs:
        hf = hf_r.next()
        init = 0.0 if prev is None else prev
        P.scan(hf[0:64, 0:n], Aa[0:64, t0:t0 + n], Uu[0:64, t0:t0 + n], init, ALU.mult, ALU.add)
        P.dma(h_d[0:64, t0:t0 + n], hf[0:64, 0:n])
        prev = hf[0:64, n - 1:n]
    prev = None
    for (t0, n) in [segs[0]] + segs[:0:-1]:
        hb = hb_r.next()
        init = 0.0 if prev is None else prev
        P.scan(hb[64:128, 0:n][:, ::-1], Aa[64:128, t0:t0 + n][:, ::-1], Uu[64:128, t0:t0 + n][:, ::-1],
               init, ALU.mult, ALU.add)
        P.dma(h_d[64:128, t0:t0 + n], hb[64:128, 0:n], q="act")
        prev = hb[64:128, 0:1]
    P.emit()
    return nc


def run_B2a(l, A, inp):
    nc = build_B2a()
    xr = A["o_lr"]
    maps = []
    for c in range(NCORES):
        b = c // 2
        off = (c % 2) * 64
        ch = slice(b * 128 + off, b * 128 + off + 64)
        xblk = np.ascontiguousarray(xr[:, b * 128:(b + 1) * 128].T)
        x64 = xr[:, ch].T
        cw = np.concatenate([inp["lru_conv_w"][l], inp["lru_conv_b"][l][None]], axis=0).T
        pp = np.zeros((128, 4), np.float32)
        wr = np.zeros((128, 128), np.float32)
        wi = np.zeros((128, 128), np.float32)
        for d in range(2):
            pp[d * 64:(d + 1) * 64, 0] = inp["lru_br"][l][d][ch]
            pp[d * 64:(d + 1) * 64, 1] = inp["lru_bi"][l][d][ch]
            pp[d * 64:(d + 1) * 64, 2] = inp["lru_lam"][l][d][ch]
            wr[:, d * 64:(d + 1) * 64] = inp["lru_wr"][l][d][b][:, off:off + 64]
            wi[:, d * 64:(d + 1) * 64] = inp["lru_wi"][l][d][b][:, off:off + 64]
        maps.append({
            "xblk": xblk, "xdup": np.ascontiguousarray(np.concatenate([x64, x64], axis=0)),
            "cwb": np.ascontiguousarray(cw[b * 128:(b + 1) * 128]),
            "cwd": np.ascontiguousarray(np.concatenate([cw[ch], cw[ch]], axis=0)),
            "wr": wr, "wi": wi, "pp": pp,
        })
    res = run_bass_kernel_spmd(nc, maps, core_ids=list(range(NCORES)))
    hf = np.concatenate([r["h"][0:64].T for r in res.results], axis=1)
    hb = np.concatenate([r["h"][64:128].T for r in res.results], axis=1)
    return {"hf": np.ascontiguousarray(hf), "hb": np.ascontiguousarray(hb)}


def build_B2b():
    nc, P = new_prog()
    zs_d = P.dram("zs", [128, 64, 2, 128], BF16, "ExternalInput")
    zc_d = P.dram("zc", [128, 2, 2, 64], BF16, "ExternalInput")
    cs1_d = P.dram("cs1", [128, 256], BF16, "ExternalInput")
    cs2_d = P.dram("cs2", [128, 256], BF16, "ExternalInput")
    tw_d = P.dram("tw", [128, 2, 128], F32, "ExternalInput")
    c128_d = P.dram("c128", [128, 128], BF16, "ExternalInput")
    s128_d = P.dram("s128", [128, 128], BF16, "ExternalInput")
    c256_d = P.dram("c256", [128, 2, 256], BF16, "ExternalInput")
    s256_d = P.dram("s256", [128, 2, 256], BF16, "ExternalInput")
    fo_d = P.dram("fo", [128, 64, 128], F32, "ExternalOutput")
    fc_d = P.dram("fc", [CTX, 64], F32, "ExternalOutput")

    zs = P.sb("zs_s", [128, 64, 2, 128], BF16)
    zc = P.sb("zc_s", [128, 2, 2, 64], BF16)
    cs1 = P.sb("cs1_s", [128, 256], BF16)
    cs2 = P.sb("cs2_s", [128, 256], BF16)
    tw = P.sb("tw_s", [128, 2, 128], F32)
    c128 = P.sb("c128_s", [128, 128], BF16)
    s128 = P.sb("s128_s", [128, 128], BF16)
    c256 = P.sb("c256_s", [128, 2, 256], BF16)
    s256 = P.sb("s256_s", [128, 2, 256], BF16)
    for (a, b_) in ((zs, zs_d), (zc, zc_d), (cs1, cs1_d), (cs2, cs2_d), (tw, tw_d), (c128, c128_d),
                    (s128, s128_d), (c256, c256_d), (s256, s256_d)):
        P.dma(a, b_)
    y2 = [P.sb(f"y2_{i}", [128, 2, 4, 128], BF16) for i in range(16)]
    pY = Rot([P.ps(f"pY{i}", [128, 256], F32) for i in range(4)])
    pZ = Rot([P.ps(f"pZ{i}", [128, 512], F32) for i in range(2)])
    pC = Rot([P.ps(f"pC{i}", [128, 64], F32) for i in range(2)])
    t_r = Rot([P.sb(f"ft{i}", [128, 128], F32) for i in range(8)])
    fo_r = Rot([P.sb(f"fo{i}", [128, 4, 128], F32) for i in range(2)])
    fc_r = Rot([P.sb(f"fc{i}", [128, 64], F32) for i in range(2)])
    Tc = tw[:, 0, :]
    Ts = tw[:, 1, :]
    for ch in range(64):
        ps = pY.next()
        P.mm(ps, zs[:, ch, 0, :], cs1, start=True, stop=False)
        P.mm(ps, zs[:, ch, 1, :], cs2, start=False, stop=True)
        yre = ps[:, 0:128]
        yim = ps[:, 128:256]
        a, b_, c, d = t_r.next(), t_r.next(), t_r.next(), t_r.next()
        P.tt(a, yre, Tc, ALU.mult)
        P.tt(b_, yim, Ts, ALU.mult)
        P.tt(c, yim, Tc, ALU.mult)
        P.tt(d, yre, Ts, ALU.mult)
        yt = y2[ch // 4]
        P.tt(yt[:, 0, ch % 4, :], a, b_, ALU.add)
        P.tt(yt[:, 1, ch % 4, :], c, d, ALU.subtract)
    scale = 1.0 / float(np.sqrt(SEQ * 128.0))
    for cg in range(16):
        po = pZ.next()
        yt = y2[cg]
        P.mm(po, c128, yt[:, 0, :, :], start=True, stop=False)
        P.mm(po, s128, yt[:, 1, :, :], start=False, stop=True)
        fo = fo_r.next()
        P.act(fo, po.re("p (c n) -> p c n", n=128), AF.Identity, scale=scale)
        P.dma(fo_d[:, cg * 4:(cg + 1) * 4, :], fo)
    cscale = 1.0 / float(np.sqrt(CTX * 128.0))
    for nt in range(2):
        pc = pC.next()
        k = 0
        for kt in range(2):
            for (m, r) in ((c256, 0), (s256, 1)):
                P.mm(pc, m[:, kt, nt * 128:(nt + 1) * 128], zc[:, kt, r, :], start=(k == 0), stop=(k == 3))
                k += 1
        fc = fc_r.next()
        P.act(fc, pc, AF.Identity, scale=cscale)
        P.dma(fc_d[nt * 128:(nt + 1) * 128, :], fc)
    P.emit()
    return nc


def fft_consts():
    K = consts()
    if "cs1" not in K:
        C, S = K["C128"], K["S128"]
        K["cs1"] = np.concatenate([C, -S], axis=1).astype(NPBF)
        K["cs2"] = np.concatenate([S, C], axis=1).astype(NPBF)
        K["c128"] = C.astype(NPBF)
        K["s128"] = S.astype(NPBF)
        k = np.arange(128)
        th = 2.0 * np.pi * np.outer(k, k) / float(SEQ)
        K["tw"] = np.ascontiguousarray(np.stack([np.cos(th), np.sin(th)], axis=1).astype(np.float32))
        n = np.arange(CTX)
        th = 2.0 * np.pi * np.outer(n, n) / float(CTX)
        K["c256"] = np.ascontiguousarray(np.cos(th).reshape(2, 128, CTX).transpose(1, 0, 2)).astype(NPBF)
        K["s256"] = np.ascontiguousarray(np.sin(th).reshape(2, 128, CTX).transpose(1, 0, 2)).astype(NPBF)
    return K


def run_B2b(l, A):
    K = fft_consts()
    nc = build_B2b()
    z = A["o_z"].reshape(NK, 4, 2, 128)
    maps = []
    for c in range(NCORES):
        g = c // 2
        off = (c % 2) * 64
        zl = z[CTX:, g, :, off:off + 64]
        zs = np.ascontiguousarray(zl.reshape(128, 128, 2, 64).transpose(0, 3, 2, 1))
        zcx = z[:CTX, g, :, off:off + 64]
        zc = np.ascontiguousarray(zcx.reshape(2, 128, 2, 64).transpose(1, 0, 2, 3))
        maps.append({"zs": zs, "zc": zc, "cs1": K["cs1"], "cs2": K["cs2"], "tw": K["tw"],
                     "c128": K["c128"], "s128": K["s128"], "c256": K["c256"], "s256": K["s256"]})
    res = run_bass_kernel_spmd(nc, maps, core_ids=list(range(NCORES)))
    lat = np.concatenate([r["fo"].transpose(0, 2, 1).reshape(SEQ, 64) for r in res.results], axis=1)
    ctx = np.concatenate([r["fc"] for r in res.results], axis=1)
    return np.ascontiguousarray(np.concatenate([ctx, lat], axis=0))


NFF = DFF // 128


def build_C():
    nc, P = new_prog()
    x_d = P.dram("x", [NTOK, D], F32, "ExternalInput")
    mix_d = P.dram("mix", [NTOK, D], F32, "ExternalInput")
    hb_d = P.dram("hb", [NTOK, 512], F32, "ExternalInput")
    gg_d = P.dram("gg", [NTOK, 512], F32, "ExternalInput")
    wout_d = P.dram("wout", [D, D], BF16, "ExternalInput")
    wg_d = P.dram("wg", [NFF, 128, 16 * 128], BF16, "ExternalInput")
    wu_d = P.dram("wu", [NFF, 128, 16 * 128], BF16, "ExternalInput")
    wd_d = P.dram("wd", [DFF, D], BF16, "ExternalInput")
    bc_d = P.dram("bc", [2, 6, 128, D], F32, "ExternalInput")
    mc_d = P.dram("mc", [128, 2, 2, 16], F32, "ExternalInput")
    lc_d = P.dram("lc", [128, 2, 16], F32, "ExternalInput")
    idb_d = P.dram("idb", [128, 128], BF16, "ExternalInput")
    out_d = P.dram("out", [NTOK, D], F32, "ExternalOutput")

    idb = P.sb("idb_s", [128, 128], BF16)
    mc = P.sb("mc_s", [128, 2, 2, 16], F32)
    lc = P.sb("lc_s", [128, 2, 16], F32)
    P.dma(idb, idb_d)
    P.dma(mc, mc_d)
    P.dma(lc, lc_d)
    fs = P.sb("fs_s", [128, 2, 2, 16], F32)
    P.ts(mc[:, :, 0, :], mc[:, :, 0, :], 1.0, None, ALU.add)
    for j in range(2):
        P.tt(fs[:, j, 0, :], lc[:, 0, :], mc[:, j, 0, :], ALU.mult)
        P.tt(fs[:, j, 1, :], lc[:, 1, :], mc[:, j, 0, :], ALU.mult)
        P.tt(fs[:, j, 1, :], fs[:, j, 1, :], mc[:, j, 1, :], ALU.add)
    bcs = [P.sb(f"bc{i}", [128, D], F32) for i in range(6)]
    G = 256
    mixT_r = Rot([P.sb("mixT", [128, 16, G], BF16)])
    hT_r = Rot([P.sb("hT", [128, 16, G], BF16)])
    actT = P.sb("actT", [128, NFF, G], BF16)
    xmid = [P.sb(f"xmid{i}", [128, D], F32) for i in range(2)]
    xn_r = Rot([P.sb(f"xn{i}", [128, D], F32) for i in range(1)])
    mt_r = Rot([P.sb(f"mt{i}", [128, D], F32) for i in range(1)])
    xt_r = mt_r
    sm_r = Rot([P.sb(f"smm{i}", [128, 512], F32) for i in range(4)])
    mb_r = Rot([P.sb(f"mb{i}", [128, D], BF16) for i in range(1)])
    wo_r = Rot([P.sb(f"wo{i}", [128, 512], BF16) for i in range(6)])
    wgu_r = Rot([P.sb(f"wgu{i}", [128, 16, 128], BF16) for i in range(4)])
    wd_r = Rot([P.sb(f"wdd{i}", [128, 11, 512], BF16) for i in range(4)])
    sl_r = Rot([P.sb(f"sl{i}", [128, G], F32) for i in range(2)])
    st_r = Rot([P.sb(f"cst{i}", [128, 4, 6], F32) for i in range(2)])
    mv_r = Rot([P.sb(f"mv{i}", [128, 8], F32) for i in range(2)])
    pT = Rot([P.ps(f"pT{i}", [128, 4, 128], BF16) for i in range(2)])
    pA = Rot([P.ps(f"pA{i}", [128, 512], F32) for i in range(2)])
    pG = Rot([P.ps(f"pG{i}", [128, G], F32) for i in range(2)])
    pUu = Rot([P.ps(f"pUu{i}", [128, G], F32) for i in range(2)])

    def layernorm(r, dst_xn):
        st = st_r.next()
        mv = mv_r.next()
        for c in range(4):
            P._rec("dve", lambda e, c=c: e.bn_stats(st.ap[:, c, :], r.ap[:, c * 512:(c + 1) * 512]), [r], [st])
        P._rec("dve", lambda e: e.bn_aggr(mv.ap[:, 0:2], st.ap), [st], [mv])
        P.ts(mv[:, 2:3], mv[:, 1:2], LN_EPS, None, ALU.add)
        P.act(mv[:, 3:4], mv[:, 2:3], AF.Sqrt)
        P.recip(mv[:, 4:5], mv[:, 3:4])
        P.stt(mv[:, 5:6], mv[:, 0:1], -1.0, mv[:, 4:5], ALU.mult, ALU.mult)
        P.act(dst_xn, r, AF.Identity, scale=mv[:, 4:5], bias=mv[:, 5:6])

    def transposes_to(src_b, dstT, ti, scale_bias=None):
        for q4 in range(4):
            pt = pT.next()
            for i in range(4):
                kc = q4 * 4 + i
                P.tr(pt[:, i, :], src_b[:, kc * 128:(kc + 1) * 128], idb)
            if scale_bias is None:
                P.copy(dstT[:, q4 * 4:(q4 + 1) * 4, ti * 128:(ti + 1) * 128], pt, eng="act")
            else:
                sc_, bi_ = scale_bias
                for i in range(4):
                    kc = q4 * 4 + i
                    P.act(dstT[:, kc, ti * 128:(ti + 1) * 128], pt[:, i, :], AF.Identity,
                          scale=sc_[:, kc:kc + 1], bias=bi_[:, kc:kc + 1])

    groups = [(g * G, G, 0) for g in range(TL // G)] + [(TL, CTX, 1)]
    cur_j = None
    for (tok0, Gn, j) in groups:
        nt = Gn // 128
        if cur_j != j:
            for i in range(6):
                P.dma(bcs[i], bc_d[j, i])
            cur_j = j
        g1b, l1g, l1b, g2b, l2g, l2b = bcs
        mixT = mixT_r.next()
        hT = hT_r.next()
        for ti in range(nt):
            r0 = tok0 + ti * 128
            mt = mt_r.next()
            P.dma(mt, mix_d[r0:r0 + 128, :])
            hbt = sm_r.next()
            ggt = sm_r.next()
            P.dma(hbt, hb_d[r0:r0 + 128, :], q="act")
            P.dma(ggt, gg_d[r0:r0 + 128, :], q="act")
            P.tt(mt[:, 512:1024], mt[:, 512:1024], hbt, ALU.add)
            P.tt(mt[:, 512:1024], mt[:, 512:1024], ggt, ALU.mult)
            mb = mb_r.next()
            P.copy(mb, mt, eng="pool")
            transposes_to(mb, mixT, ti)
        for ti in range(nt):
            r0 = tok0 + ti * 128
            xt = xt_r.next()
            P.dma(xt, x_d[r0:r0 + 128, :])
            r = xn_r.next()
            for cc in range(4):
                pa = pA.next()
                for kc in range(16):
                    wo = wo_r.next()
                    P.dma(wo[:, 0:512], wout_d[kc * 128:(kc + 1) * 128, cc * 512:(cc + 1) * 512],
                          q=("sp" if kc % 2 == 0 else "pool"))
                    P.mm(pa, mixT[:, kc, ti * 128:(ti + 1) * 128], wo[:, 0:512], start=(kc == 0), stop=(kc == 15))
                tmp = sm_r.next()
                P.tt(tmp, pa, g1b[:, cc * 512:(cc + 1) * 512], ALU.mult)
                P.stt(r[:, cc * 512:(cc + 1) * 512], xt[:, cc * 512:(cc + 1) * 512], ALPHA, tmp, ALU.mult, ALU.add)
            xm = xmid[ti]
            layernorm(r, xm)
            mb = mb_r.next()
            P.copy(mb, xm, eng="pool")
            transposes_to(mb, hT, ti, scale_bias=(fs[:, j, 0, :], fs[:, j, 1, :]))
            P.tt(xm, xm, l1g, ALU.mult)
            P.tt(xm, xm, l1b, ALU.add)
        for f in range(NFF):
            wgt = wgu_r.next()
            wut = wgu_r.next()
            P.dma(wgt, wg_d[f].re("p (kc j) -> p kc j", j=128), q="sp")
            P.dma(wut, wu_d[f].re("p (kc j) -> p kc j", j=128), q="pool")
            pg = pG.next()
            pu = pUu.next()
            for kc in range(16):
                P.mm(pg[:, 0:Gn], wgt[:, kc, :], hT[:, kc, 0:Gn], start=(kc == 0), stop=(kc == 15))
            for kc in range(16):
                P.mm(pu[:, 0:Gn], wut[:, kc, :], hT[:, kc, 0:Gn], start=(kc == 0), stop=(kc == 15))
            sl = sl_r.next()
            P.act(sl[:, 0:Gn], pg[:, 0:Gn], AF.Silu)
            P.tt(actT[:, f, 0:Gn], sl[:, 0:Gn], pu[:, 0:Gn], ALU.mult)
        for cc in range(4):
            wds = []
            for q in range(4):
                wdt = wd_r.next()
                P.dma(wdt, wd_d[q * 11 * 128:(q + 1) * 11 * 128, cc * 512:(cc + 1) * 512].re("(f p) n -> p f n", p=128),
                      q=("sp" if q % 2 == 0 else "pool"))
                wds.append(wdt)
            for ti in range(nt):
                pa = pA.next()
                for f in range(NFF):
                    P.mm(pa, actT[:, f, ti * 128:(ti + 1) * 128], wds[f // 11][:, f % 11, :],
                         start=(f == 0), stop=(f == NFF - 1))
                tmp = sm_r.next()
                P.tt(tmp, pa, g2b[:, cc * 512:(cc + 1) * 512], ALU.mult)
                xm = xmid[ti]
                P.stt(xm[:, cc * 512:(cc + 1) * 512], xm[:, cc * 512:(cc + 1) * 512], ALPHA, tmp, ALU.mult, ALU.add)
        for ti in range(nt):
            r0 = tok0 + ti * 128
            xm = xmid[ti]
            o = xn_r.next()
            layernorm(xm, o)
            P.tt(o, o, l2g, ALU.mult)
            P.tt(o, o, l2b, ALU.add)
            P.dma(out_d[r0:r0 + 128, :], o)
    P.emit()
    return nc


def run_C(l, x_lat, x_ctx, parts, m, inp, wb):
    K = consts()
    nc = build_C()
    bc = np.zeros((2, 6, 128, D), np.float32)
    mc = np.zeros((128, 2, 2, 16), np.float32)
    for j in range(2):
        for i, v in enumerate((m[j, l, 2], inp["ln1_g"][l], inp["ln1_b"][l], m[j, l, 5], inp["ln2_g"][l], inp["ln2_b"][l])):
            bc[j, i] = v[None, :]
        mc[:, j, 0, :] = col_layout(m[j, l, 4])
        mc[:, j, 1, :] = col_layout(m[j, l, 3])
    lc = np.ascontiguousarray(np.stack([col_layout(inp["ln1_g"][l]), col_layout(inp["ln1_b"][l])], axis=1))
    wg, wu, wd, wout = wb[l]["wg"], wb[l]["wu"], wb[l]["wd"], wb[l]["wout"]
    mix = np.concatenate([parts["da"], parts["hf"], parts["mla"], parts["fft"]], axis=1).astype(np.float32)
    maps = []
    for c in range(NCORES):
        rows = np.concatenate([np.arange(CTX + c * TL, CTX + (c + 1) * TL), np.arange(CTX)])
        maps.append({
            "x": np.ascontiguousarray(np.concatenate([x_lat[c * TL:(c + 1) * TL], x_ctx], axis=0)),
            "mix": np.ascontiguousarray(mix[rows]),
            "hb": np.ascontiguousarray(parts["hb"][rows], dtype=np.float32),
            "gg": np.ascontiguousarray(parts["gg"][rows], dtype=np.float32),
            "wout": wout, "wg": wg, "wu": wu, "wd": wd, "bc": bc, "mc": mc, "lc": lc, "idb": K["idb"],
        })
    res = run_bass_kernel_spmd(nc, maps, core_ids=list(range(NCORES)))
    lat = np.concatenate([r["out"][:TL] for r in res.results], axis=0)
    ctx = res.results[0]["out"][TL:]
    return lat, ctx


def kernel(**inputs):
    inp = {k: np.asarray(v) for k, v in inputs.items()}
    m, wb = run_Z(inp)
    x_lat = np.ascontiguousarray(inp["x"][0], dtype=np.float32)
    x_ctx = np.ascontiguousarray(inp["ctx"][0], dtype=np.float32)
    for l in range(2):
        A = run_A(l, x_lat, x_ctx, m, inp)
        B1 = run_B1(l, A, inp)
        H = run_B2a(l, A, inp)
        F = run_B2b(l, A)
        parts = {"da": B1["o_da"], "hf": H["hf"], "hb": H["hb"], "gg": A["o_gg"],
                 "mla": B1["o_mla"], "fft": F}
        x_lat, x_ctx = run_C(l, x_lat, x_ctx, parts, m, inp, wb)
    return np.ascontiguousarray(x_lat[None], dtype=np.float32)
```
